# Optimizing a Trainium2 kernel written in Bass

```python
import math
import jax, jax.numpy as jnp
from jax import lax
import numpy as np

D_MODEL = 1024
BATCH = 4
SEQ = 4096
DEPTH = 4

CHUNK = 64
N_MIXERS = 3
EPS = 1e-6

POOL_WINDOWS = (2, 4, 8, 16)
N_POOL_GROUPS = len(POOL_WINDOWS)
POOL_GROUP = D_MODEL // N_POOL_GROUPS

SB_HEAD_DIM = 64
SB_HEADS = D_MODEL // SB_HEAD_DIM
Q_BLOCK = 128

S5_GROUP = 16
S5_GROUPS = D_MODEL // S5_GROUP
S5_STATE = 64
S5_DT_MIN = 1e-3
S5_DT_MAX = 1e-1

MEM_LEN = 256
XA_HEADS = 4
XA_HEAD_DIM = D_MODEL // XA_HEADS

D_FF = ((8 * D_MODEL // 3 + 127) // 128) * 128
CONV_WIDTH = 3

N_POOL_LAYERS = (DEPTH + 2) // 3
N_SB_LAYERS = (DEPTH + 1) // 3
N_S5_LAYERS = DEPTH // 3

kernel_name = "hybrid_pool_stickbreak_s5_streaming_trunk"


def rms_norm(x, g):
    xf = x.astype(jnp.float32)
    y = xf * lax.rsqrt(jnp.mean(xf * xf, axis=-1, keepdims=True) + EPS)
    return (y * g.astype(jnp.float32)).astype(x.dtype)


def causal_shift(x, k):
    return jnp.pad(x, ((0, 0), (k, 0), (0, 0)))[:, : x.shape[1]]


def pool_mixer(h, w, scale):
    B, S, D = h.shape
    hf = h.astype(jnp.float32)
    cs = jnp.cumsum(hf, axis=1)
    counts = jnp.arange(1, S + 1, dtype=jnp.float32)
    groups = []
    for gi, win in enumerate(POOL_WINDOWS):
        sl = slice(gi * POOL_GROUP, (gi + 1) * POOL_GROUP)
        c = cs[..., sl]
        win_sum = c - causal_shift(c, win)
        cnt = jnp.minimum(counts, float(win))[None, :, None]
        groups.append(win_sum / cnt - hf[..., sl])
    p = jnp.stack(groups, axis=2).astype(h.dtype)
    y = jnp.einsum('bsgc,gcd->bsgd', p, w).reshape(B, S, D)
    return y * scale


def stick_breaking_attention(h, w_qkv, w_o):
    B, S, D = h.shape
    qkv = (h @ w_qkv).reshape(B, S, 3, SB_HEADS, SB_HEAD_DIM)
    q, k, v = qkv[:, :, 0], qkv[:, :, 1], qkv[:, :, 2]
    scale = SB_HEAD_DIM ** -0.5
    outs = []
    for i in range(S // Q_BLOCK):
        q0 = i * Q_BLOCK
        kend = q0 + Q_BLOCK
        z = jnp.einsum('bqhd,bkhd->bhqk', q[:, q0:kend], k[:, :kend]).astype(jnp.float32) * scale
        qpos = q0 + jnp.arange(Q_BLOCK)
        kpos = jnp.arange(kend)
        mask = kpos[None, :] < qpos[:, None]
        log_one_minus = jnp.where(mask, jax.nn.log_sigmoid(-z), 0.0)
        between = lax.cumsum(log_one_minus, axis=3, reverse=True) - log_one_minus
        a = jnp.where(mask, jnp.exp(jax.nn.log_sigmoid(z) + between), 0.0)
        outs.append(jnp.einsum('bhqk,bkhd->bqhd', a.astype(v.dtype), v[:, :kend]))
    o = jnp.concatenate(outs, axis=1).reshape(B, S, D)
    return o @ w_o


def _lin_rec_combine(left, right):
    a1, b1 = left
    a2, b2 = right
    return a1 * a2, a2 * b1 + b2


def s5_mixer(h, a_re, a_im, log_dt, b_re, b_im, c_re, c_im, d, w_glu):
    B, S, D = h.shape
    f32 = jnp.float32
    u = h.astype(f32)
    lam = lax.complex(a_re.astype(f32), a_im.astype(f32))
    dt = jnp.exp(log_dt.astype(f32))[:, None]
    dt_lam = lam * dt
    a_bar = jnp.exp(dt_lam)
    b_bar = ((a_bar - 1.0) / lam)[..., None] * lax.complex(b_re.astype(f32), b_im.astype(f32))
    c_mat = lax.complex(c_re.astype(f32), c_im.astype(f32))
    steps = jnp.arange(1, CHUNK + 1, dtype=f32)
    a_pow = jnp.exp(steps[:, None, None] * dt_lam[None])
    n_chunks = S // CHUNK
    u_chunks = u.reshape(B, n_chunks, CHUNK, S5_GROUPS, S5_GROUP).transpose(1, 0, 2, 3, 4)

    def chunk_step(state, u_blk):
        bu = jnp.einsum('bcgi,gpi->bcgp', u_blk.astype(jnp.complex64), b_bar)
        a_seq = jnp.broadcast_to(a_bar, bu.shape)
        _, xs = lax.associative_scan(_lin_rec_combine, (a_seq, bu), axis=1)
        xs = xs + a_pow[None] * state[:, None]
        y = jnp.real(jnp.einsum('bcgp,gip->bcgi', xs, c_mat))
        return xs[:, -1], y

    state0 = jnp.zeros((B, S5_GROUPS, S5_STATE), jnp.complex64)
    _, ys = lax.scan(chunk_step, state0, u_chunks)
    y = ys.transpose(1, 0, 2, 3, 4).reshape(B, S, D) + d.astype(f32) * u
    y = jax.nn.gelu(y).astype(h.dtype)
    val, gate = jnp.split(y @ w_glu, 2, axis=-1)
    return val * jax.nn.sigmoid(gate)


def memory_cross_attention(h, mem_n, wq, wkv, wo):
    B, S, D = h.shape
    M = mem_n.shape[1]
    q = (h @ wq).reshape(B, S, XA_HEADS, XA_HEAD_DIM)
    kv = (mem_n @ wkv).reshape(B, M, 2, XA_HEADS, XA_HEAD_DIM)
    k, v = kv[:, :, 0], kv[:, :, 1]
    s = jnp.einsum('bshd,bmhd->bhsm', q, k).astype(jnp.float32) * (XA_HEAD_DIM ** -0.5)
    p = jax.nn.softmax(s, axis=-1).astype(v.dtype)
    o = jnp.einsum('bhsm,bmhd->bshd', p, v).reshape(B, S, D)
    return o @ wo


def conv_glu_ffn(h, w_up, conv_w, conv_b, w_down):
    u = h @ w_up
    u = sum(conv_w[CONV_WIDTH - 1 - k] * causal_shift(u, k) for k in range(CONV_WIDTH)) + conv_b
    val, gate = jnp.split(u, 2, axis=-1)
    return (jax.nn.silu(gate) * val) @ w_down


def setup_inputs(seed: int = 0) -> dict:
    key = jax.random.key(seed)
    keys = iter(jax.random.split(key, 40))
    f32 = jnp.float32

    def nrm(shape, scale):
        return jax.random.normal(next(keys), shape, f32) * scale

    def gain(shape):
        return 1.0 + nrm(shape, 0.02)

    D, F = D_MODEL, D_FF
    G, P, Cg = S5_GROUPS, S5_STATE, S5_GROUP
    a_im_init = jnp.pi * jnp.arange(P, dtype=f32)
    return {
        "x": nrm((BATCH, SEQ, D), 1.0),
        "mem": nrm((BATCH, MEM_LEN, D), 1.0),
        "mix_norm_g": gain((DEPTH, D)),
        "pool_w": nrm((N_POOL_LAYERS, N_POOL_GROUPS, POOL_GROUP, POOL_GROUP), POOL_GROUP ** -0.5),
        "pool_scale": gain((N_POOL_LAYERS, D)),
        "sb_w_qkv": nrm((N_SB_LAYERS, D, 3 * D), D ** -0.5),
        "sb_w_o": nrm((N_SB_LAYERS, D, D), D ** -0.5),
        "s5_a_re": -0.5 + nrm((N_S5_LAYERS, G, P), 0.01),
        "s5_a_im": a_im_init + nrm((N_S5_LAYERS, G, P), 0.01),
        "s5_log_dt": jax.random.uniform(next(keys), (N_S5_LAYERS, G), f32,
                                        minval=math.log(S5_DT_MIN), maxval=math.log(S5_DT_MAX)),
        "s5_b_re": nrm((N_S5_LAYERS, G, P, Cg), (2 * Cg) ** -0.5),
        "s5_b_im": nrm((N_S5_LAYERS, G, P, Cg), (2 * Cg) ** -0.5),
        "s5_c_re": nrm((N_S5_LAYERS, G, Cg, P), (2 * P) ** -0.5),
        "s5_c_im": nrm((N_S5_LAYERS, G, Cg, P), (2 * P) ** -0.5),
        "s5_d": nrm((N_S5_LAYERS, D), 1.0),
        "s5_w_glu": nrm((N_S5_LAYERS, D, 2 * D), D ** -0.5),
        "xa_norm_g": gain((DEPTH, D)),
        "mem_norm_g": gain((DEPTH, D)),
        "xa_wq": nrm((DEPTH, D, D), D ** -0.5),
        "xa_wkv": nrm((DEPTH, D, 2 * D), D ** -0.5),
        "xa_wo": nrm((DEPTH, D, D), D ** -0.5),
        "ffn_norm_g": gain((DEPTH, D)),
        "ffn_w_up": nrm((DEPTH, D, 2 * F), D ** -0.5),
        "ffn_conv_w": nrm((DEPTH, CONV_WIDTH, 2 * F), CONV_WIDTH ** -0.5),
        "ffn_conv_b": nrm((DEPTH, 2 * F), 0.01),
        "ffn_w_down": nrm((DEPTH, F, D), F ** -0.5),
        "final_norm_g": gain((D,)),
    }


def reference(x, mem, mix_norm_g, pool_w, pool_scale, sb_w_qkv, sb_w_o,
              s5_a_re, s5_a_im, s5_log_dt, s5_b_re, s5_b_im, s5_c_re, s5_c_im, s5_d, s5_w_glu,
              xa_norm_g, mem_norm_g, xa_wq, xa_wkv, xa_wo,
              ffn_norm_g, ffn_w_up, ffn_conv_w, ffn_conv_b, ffn_w_down, final_norm_g):
    h = x
    for i in range(DEPTH):
        kind = i % N_MIXERS
        j = i // N_MIXERS
        hn = rms_norm(h, mix_norm_g[i])
        if kind == 0:
            t = pool_mixer(hn, pool_w[j], pool_scale[j])
        elif kind == 1:
            t = stick_breaking_attention(hn, sb_w_qkv[j], sb_w_o[j])
        else:
            t = s5_mixer(hn, s5_a_re[j], s5_a_im[j], s5_log_dt[j], s5_b_re[j], s5_b_im[j],
                         s5_c_re[j], s5_c_im[j], s5_d[j], s5_w_glu[j])
        h = h + t.astype(h.dtype)
        m = memory_cross_attention(rms_norm(h, xa_norm_g[i]), rms_norm(mem, mem_norm_g[i]),
                                   xa_wq[i], xa_wkv[i], xa_wo[i])
        h = h + m.astype(h.dtype)
        f = conv_glu_ffn(rms_norm(h, ffn_norm_g[i]), ffn_w_up[i], ffn_conv_w[i], ffn_conv_b[i], ffn_w_down[i])
        h = h + f.astype(h.dtype)
    return rms_norm(h, final_norm_g)
```

```python
import contextlib
import numpy as np
import ml_dtypes
import concourse.bass as bass
import concourse.mybir as mybir
from concourse.bass_utils import run_bass_kernel_spmd

F32 = mybir.dt.float32
BF16 = mybir.dt.bfloat16
AF = mybir.ActivationFunctionType
ALU = mybir.AluOpType
NPBF = ml_dtypes.bfloat16

D = 1024
SEQ = 4096
BATCH = 4
DFF = 2816
MEM = 256
EPS = 1e-6
NCORES = 8
HT = 128
PH = 16
OWN = 2048

SAME_ENG_SYNC = True


class T:
    __slots__ = ("name", "lw", "rd")

    def __init__(self, name):
        self.name = name
        self.lw = None
        self.rd = []


class I:
    __slots__ = ("eng", "fn", "dma", "deps", "signal", "ev", "idx")

    def __init__(self, eng, fn, dma):
        self.eng = eng
        self.fn = fn
        self.dma = dma
        self.deps = []
        self.signal = False
        self.ev = None


class Sched:
    ENGS = ("pe", "act", "dve", "pool", "sp")
    KDMA = 8

    def __init__(self, nc):
        self.nc = nc
        self.q = {e: [] for e in self.ENGS}
        self.outs = []

    def add(self, eng, fn, reads=(), writes=(), dma=False):
        ins = I(eng, fn, dma)
        deps = {}
        for t in reads:
            if t.lw is not None:
                deps[id(t.lw)] = t.lw
        for t in writes:
            if t.lw is not None:
                deps[id(t.lw)] = t.lw
            for r in t.rd:
                deps[id(r)] = r
        for d in deps.values():
            if d is ins:
                continue
            if not d.dma and d.eng == eng and not dma:
                if eng == "pe" or not SAME_ENG_SYNC:
                    continue
            d.signal = True
            ins.deps.append(d)
        for t in reads:
            t.rd.append(ins)
        for t in writes:
            t.lw = ins
            t.rd = []
        self.q[eng].append(ins)
        return ins

    def emit(self, stack):
        nc = self.nc
        sems = {}
        csem = {}
        for e in ("pe", "act", "dve", "pool"):
            cnt = sum(1 for i in self.q[e] if (i.signal and not i.dma))
            nsem = max(1, (cnt + 29999) // 30000)
            csem[e] = [stack.enter_context(nc.semaphore(f"s_{e}{k}")) for k in range(nsem)]
        dsem = {}
        for e in ("sp", "pool", "act"):
            if any(i.dma for i in self.q[e]):
                dsem[e] = [stack.enter_context(nc.semaphore(f"d_{e}{k}")) for k in range(self.KDMA)]
        for e in self.ENGS:
            c = 0
            nd = 0
            for ins in self.q[e]:
                if ins.dma:
                    ins.ev = (dsem[e][nd % self.KDMA], 16 * (nd // self.KDMA + 1), nd)
                    nd += 1
                elif ins.signal:
                    ins.ev = (csem[e][c // 30000], c % 30000 + 1, None)
                    c += 1
        block = stack.enter_context(nc.Block())

        def run(engname, eng):
            seen = {}

            def wait(sem, val):
                k = id(sem)
                if seen.get(k, 0) >= val:
                    return
                eng.wait_ge(sem, val)
                seen[k] = val

            for ins in self.q[engname]:
                for d in ins.deps:
                    wait(d.ev[0], d.ev[1])
                if ins.dma:
                    sem, val, nd = ins.ev
                    if val > 16:
                        wait(sem, val - 16)
                    ins.fn(eng).then_inc(sem, 16)
                else:
                    r = ins.fn(eng)
                    if ins.signal:
                        r.then_inc(ins.ev[0], 1)

        @block.tensor
        def _(e):
            run("pe", e)

        @block.scalar
        def _(e):
            run("act", e)

        @block.vector
        def _(e):
            run("dve", e)

        @block.gpsimd
        def _(e):
            run("pool", e)

        @block.sync
        def _(e):
            run("sp", e)
            seen = {}
            for ins in self.outs:
                sem, val, _ = ins.ev
                if seen.get(id(sem), 0) < val:
                    e.wait_ge(sem, val)
                    seen[id(sem)] = val


class KB:
    def __init__(self):
        self.nc = bass.Bass("TRN2", target_bir_lowering=False)
        self.stack = contextlib.ExitStack()
        self.s = Sched(self.nc)
        self.nbank = 0
        self.banks = []

    def dram(self, name, shape, dt, out=False):
        return self.nc.dram_tensor(name, list(shape), dt, kind="ExternalOutput" if out else "ExternalInput").ap()

    def sb(self, name, shape, dt):
        return self.stack.enter_context(self.nc.sbuf_tensor(name, list(shape), dt))

    def ps(self, name):
        return self.stack.enter_context(self.nc.psum_tensor(name, [128, 512], F32))

    def finish(self):
        self.s.emit(self.stack)
        self.stack.close()
        return self.nc

    def dma(self, out, in_, reads=(), writes=(), eng="sp", is_out=False):
        ins = self.s.add(eng, lambda e: e.dma_start(out=out, in_=in_), reads, writes, dma=True)
        if is_out:
            self.s.outs.append(ins)
        return ins

    def mm(self, out, lhsT, rhs, start, stop, reads=(), writes=()):
        return self.s.add("pe", lambda e: e.matmul(out, lhsT, rhs, start=start, stop=stop), reads, writes)

    def act(self, out, in_, func, reads=(), writes=(), bias=None, scale=None, eng="act"):
        kw = {}
        if bias is not None:
            kw["bias"] = bias
        if scale is not None:
            kw["scale"] = scale
        return self.s.add("act", lambda e: e.activation(out=out, in_=in_, func=func, **kw), reads, writes)

    def tt(self, eng, out, in0, in1, op, reads=(), writes=()):
        return self.s.add(eng, lambda e: e.tensor_tensor(out=out, in0=in0, in1=in1, op=op), reads, writes)

    def ts(self, eng, out, in0, s1, op0, s2=None, op1=None, reads=(), writes=()):
        if op1 is None:
            return self.s.add(eng, lambda e: e.tensor_scalar(out=out, in0=in0, scalar1=s1, scalar2=None, op0=op0),
                              reads, writes)
        return self.s.add(eng, lambda e: e.tensor_scalar(out=out, in0=in0, scalar1=s1, scalar2=s2, op0=op0, op1=op1),
                          reads, writes)

    def stt(self, out, in0, scalar, in1, op0, op1, reads=(), writes=()):
        return self.s.add("dve", lambda e: e.scalar_tensor_tensor(out=out, in0=in0, scalar=scalar, in1=in1,
                                                                  op0=op0, op1=op1), reads, writes)

    def copy(self, eng, out, in_, reads=(), writes=()):
        if eng == "act":
            return self.s.add("act", lambda e: e.activation(out=out, in_=in_, func=AF.Copy), reads, writes)
        return self.s.add(eng, lambda e: e.tensor_copy(out=out, in_=in_), reads, writes)

    def memset(self, eng, ap, val, writes=()):
        return self.s.add(eng, lambda e: e.memset(ap, val), (), writes)

    def recip(self, out, in_, reads=(), writes=()):
        return self.s.add("dve", lambda e: e.reciprocal(out=out, in_=in_), reads, writes)


PIECE = 8192


def build_cast(npieces):
    k = KB()
    wf = k.dram("wf", [128, npieces * PIECE], F32)
    wb = k.dram("wb", [128, npieces * PIECE], BF16, out=True)
    NB = 4
    bufs = [k.sb(f"cb{i}", [128, PIECE], BF16) for i in range(NB)]
    tl = [T(f"cb{i}") for i in range(NB)]
    for p in range(npieces):
        b = p % NB
        k.dma(bufs[b][:], wf[:, p * PIECE:(p + 1) * PIECE], writes=[tl[b]], eng="pool")
        k.dma(wb[:, p * PIECE:(p + 1) * PIECE], bufs[b][:], reads=[tl[b]], eng="sp", is_out=True)
    return k.finish()


_CACHE = {}


def _get(key, fn):
    if key not in _CACHE:
        _CACHE[key] = fn()
    return _CACHE[key]


def run_cast(arrs, maxp=6):
    flat = np.concatenate([a.ravel() for a in arrs])
    per = NCORES * 128 * PIECE
    total = (flat.size + per - 1) // per
    obs = []
    for p0 in range(0, total, maxp):
        npieces = min(maxp, total - p0)
        seg = flat[p0 * per:(p0 + npieces) * per]
        if seg.size < npieces * per:
            seg = np.concatenate([seg, np.zeros(npieces * per - seg.size, np.float32)])
        seg = seg.reshape(NCORES, 128, npieces * PIECE)
        nc = _get(("cast", npieces), lambda: build_cast(npieces))
        res = run_bass_kernel_spmd(nc, [{"wf": np.ascontiguousarray(seg[c])} for c in range(NCORES)],
                                   core_ids=list(range(NCORES)))
        obs.append(np.stack([np.asarray(res.results[c]["wb"]) for c in range(NCORES)]).reshape(-1))
    ob = np.concatenate(obs)
    out = []
    o = 0
    for a in arrs:
        out.append(ob[o:o + a.size].reshape(a.shape))
        o += a.size
    return out


V_MIXG, V_XAG, V_FFNG, V_MEMG, V_FING, V_NEXTG, V_PSCALE, V_S5D = 0, 8, 16, 24, 32, 40, 48, 56
V_CONVW, V_CONVB, V_FLAG, V_PCORR, NV = 64, 196, 240, 241, 384
NFC = 22


class Rot:
    def __init__(self, items):
        self.items = items
        self.i = 0

    def __call__(self):
        r = self.items[self.i % len(self.items)]
        self.i += 1
        return r


def build_layer(kind, final=False, emit_next=False):
    k = KB()
    ph = PH if kind == "pool" else 0
    NTOT = ph + HT + OWN
    NTM = HT + 512
    hin = k.dram("hin", [128, 8, NTOT], F32)
    vecd = k.dram("vecs", [128, NV], F32)
    memd = k.dram("memT", [128, 8, MEM], F32)
    wqd = k.dram("wq", [128, 8, D], BF16)
    wod = k.dram("wo", [128, 8, D], BF16)
    wkd = k.dram("wk", [128, 8, D], BF16)
    wvd = k.dram("wv", [128, 8, D], BF16)
    wupd = k.dram("wup", [11, 128, 8, 512], BF16)
    wdnd = k.dram("wdn", [8, 128, NFC, 128], BF16)
    if kind == "pool":
        pwd = k.dram("pw", [128, 4, 2, 256], BF16)
    elif kind == "sb":
        wsod = k.dram("wso", [128, 8, D], BF16)
        oind = k.dram("oin", [128, 8, HT + OWN], F32)
    else:
        wglud = k.dram("wglu", [2, 128, 8, D], BF16)
        yind = k.dram("yin", [128, 8, HT + OWN], F32)
        hnind = k.dram("hnin", [128, 8, HT + OWN], F32)
    hout = k.dram("hout", [128, 8, OWN], F32, out=True)
    if emit_next:
        hnout = k.dram("hnout", [128, 8, OWN], F32, out=True)

    hT = k.sb("hT", [128, 8, ph + NTM], F32)
    hnT = k.sb("hnT", [128, 8, NTM], BF16)
    big = k.sb("big", [128, NFC, NTM], BF16)
    rstd = k.sb("rstd", [128, ph + NTM], F32)
    vec = k.sb("vec", [128, NV], F32)
    ones_f = k.sb("ones_f", [128, 128], F32)
    ones_b = k.sb("ones_b", [128, 128], BF16)
    wA = k.sb("wA", [128, 8, D], BF16)
    wup = [k.sb(f"wup{i}", [128, 8, 512], BF16) for i in range(2)]
    wdn = [k.sb(f"wdn{i}", [128, NFC, 128], BF16) for i in range(2)]
    kT = k.sb("kT", [128, 8, MEM], BF16)
    Vm = k.sb("Vm", [128, 2, D], BF16)
    memn = k.sb("memn", [128, 8, MEM], BF16)
    sq = [k.sb(f"sq{i}", [128, 512], F32) for i in range(2)]
    accv = [k.sb(f"accv{i}", [128, 512], F32) for i in range(2)]
    accg = [k.sb(f"accg{i}", [128, 512], F32) for i in range(2)]
    sgb = [k.sb(f"sgb{i}", [128, 512], F32) for i in range(2)]
    ET = [k.sb(f"ET{i}", [128, 2, 512], BF16) for i in range(2)]
    rden = [k.sb(f"rden{i}", [128, 512], F32) for i in range(2)]
    carry = k.sb("carry", [128, 2 * NFC, 2], F32)
    fout = k.sb("fout", [128, 8, 512], F32)
    if kind == "pool":
        hnf = k.sb("hnf", [128, 8, ph + NTM], F32)
        pp = [k.sb(f"pp{i}", [128, ph + 512], F32) for i in range(2)]
        pw = k.sb("pw_sb", [128, 4, 2, 256], BF16)
    if kind == "s5":
        ys = k.sb("ys", [128, 8, NTM], F32)
        hs = k.sb("hs", [128, 8, NTM], F32)
    banks = [k.ps(f"bank{i}") for i in range(8)]

    t_h = [T(f"h{c}") for c in range(8)]
    t_hn = [T(f"hn{c}") for c in range(8)]
    t_big = [T(f"big{c}") for c in range(NFC)]
    t_rstd, t_vec, t_wA, t_kT, t_Vm, t_memn, t_fout = T("rstd"), T("vec"), T("wA"), T("kT"), T("Vm"), T("memn"), T("fout")
    t_ones = T("ones")
    t_wup = [T("wup0"), T("wup1")]
    t_wdn = [T("wdn0"), T("wdn1")]
    t_carry = [T(f"carry{c}") for c in range(2 * NFC)]
    t_hnf = [T(f"hnf{c}") for c in range(8)]
    t_pw, t_ys, t_hs = T("pw"), T("ys"), T("hs")
    bank = Rot([(banks[i], T(f"bank{i}")) for i in range(8)])
    sqr = Rot([(sq[i], T(f"sq{i}")) for i in range(2)])
    accvr = Rot([(accv[i], T(f"accv{i}")) for i in range(2)])
    accgr = Rot([(accg[i], T(f"accg{i}")) for i in range(2)])
    sgbr = Rot([(sgb[i], T(f"sgb{i}")) for i in range(2)])
    ETr = Rot([(ET[i], T(f"ET{i}")) for i in range(2)])
    rdenr = Rot([(rden[i], T(f"rden{i}")) for i in range(2)])
    if kind == "pool":
        ppr = Rot([(pp[i], T(f"pp{i}")) for i in range(2)])

    def V(col):
        return vec[:, col:col + 1]

    k.memset("pool", ones_f[:], 1.0, writes=[t_ones])
    k.memset("pool", ones_b[:], 1.0, writes=[t_ones])
    k.memset("pool", carry[:], 0.0, writes=t_carry)
    k.dma(vec[:], vecd[:, :], writes=[t_vec])

    def rms_rstd(src, ncols, src_tiles, col0=0):
        for blk in range(0, ncols, 512):
            n = min(512, ncols - blk)
            bk, tb = bank()
            for c in range(8):
                s, ts_ = sqr()
                k.act(s[:, :n], src(c, blk, n), AF.Square, reads=[src_tiles[c]], writes=[ts_])
                k.mm(bk[:, :n], ones_f[:, :], s[:, :n], c == 0, c == 7, reads=[ts_, t_ones], writes=[tb])
            k.act(rstd[:, col0 + blk:col0 + blk + n], bk[:, :n], AF.Sqrt, reads=[tb], writes=[t_rstd],
                  scale=1.0 / D, bias=V_EPS[0])
            k.recip(rstd[:, col0 + blk:col0 + blk + n], rstd[:, col0 + blk:col0 + blk + n], reads=[t_rstd],
                    writes=[t_rstd])

    epsb = k.sb("epsb", [128, 1], F32)
    t_eps = T("eps")
    k.memset("pool", epsb[:], EPS, writes=[t_eps])
    V_EPS = [epsb[:, 0:1]]

    def normalize(dst, dst_tiles, src, src_tiles, gcol, ncols, rcol0=0):
        for c in range(8):
            k.stt(dst(c, 0, ncols), src(c, 0, ncols), V(gcol + c), rstd[:, rcol0:rcol0 + ncols], ALU.mult, ALU.mult,
                  reads=[src_tiles[c], t_rstd, t_vec], writes=[dst_tiles[c]])

    k.dma(fout[:, :, 0:MEM], memd[:, :, :], writes=[t_fout])
    t_f8 = [t_fout] * 8
    rms_rstd(lambda c, lo, n: fout[:, c, lo:lo + n], MEM, t_f8)
    normalize(lambda c, lo, n: memn[:, c, lo:lo + n], [t_memn] * 8, lambda c, lo, n: fout[:, c, lo:lo + n], t_f8,
              V_MEMG, MEM)
    k.dma(wA[:], wkd[:, :, :], writes=[t_wA])
    for oc in range(8):
        bk, tb = bank()
        for kk in range(8):
            k.mm(bk[:, :MEM], wA[:, kk, oc * 128:(oc + 1) * 128], memn[:, kk, :], kk == 0, kk == 7,
                 reads=[t_wA, t_memn], writes=[tb])
        k.copy("act", kT[:, oc, :], bk[:, :MEM], reads=[tb], writes=[t_kT])
    k.dma(wA[:], wvd[:, :, :], writes=[t_wA])
    for mc in range(2):
        for blk in range(2):
            bk, tb = bank()
            for kk in range(8):
                k.mm(bk[:, :], memn[:, kk, mc * 128:(mc + 1) * 128], wA[:, kk, blk * 512:(blk + 1) * 512], kk == 0,
                     kk == 7, reads=[t_wA, t_memn], writes=[tb])
            k.copy("act", Vm[:, mc, blk * 512:(blk + 1) * 512], bk[:, :], reads=[tb], writes=[t_Vm])

    def proj_add(w_sb, t_w, rhs_chunk, rhs_tiles, tiles, nk=8):
        for oc in range(8):
            for (off, n, _h) in tiles:
                bk, tb = bank()
                for kk in range(nk):
                    k.mm(bk[:, :n], w_sb[:, kk, oc * 128:(oc + 1) * 128], rhs_chunk(kk, off, n), kk == 0,
                         kk == nk - 1, reads=[t_w, rhs_tiles[kk]], writes=[tb])
                k.tt("dve", hT[:, oc, ph + off:ph + off + n], bk[:, :n], hT[:, oc, ph + off:ph + off + n], ALU.add,
                     reads=[tb, t_h[oc]], writes=[t_h[oc]])

    for si in range(4):
        if si == 0:
            tiles = [(0, HT, True), (HT, 512, False)]
            NT = HT + 512
            col_lo = 0
            acol = 0
        else:
            tiles = [(0, 512, False)]
            NT = 512
            col_lo = HT + 512 * si
            acol = HT + 512 * si
        own_off = tiles[-1][0]
        k.dma(hT[:, :, 0:ph + NT], hin[:, :, col_lo:col_lo + ph + NT], writes=t_h)

        if kind == "pool":
            rms_rstd(lambda c, lo, n: hT[:, c, lo:lo + n], ph + NT, t_h)
            normalize(lambda c, lo, n: hnf[:, c, lo:lo + n], t_hnf, lambda c, lo, n: hT[:, c, lo:lo + n], t_h,
                      V_MIXG, ph + NT)
            if si == 0:
                k.dma(pw[:], pwd[:, :, :, :], writes=[t_pw])
            for (off, n, halo) in tiles:
                L = ph + n
                for c in range(8):
                    win = (2, 4, 8, 16)[c // 2]
                    cur, tcur = hnf[:, c, off:off + L], t_hnf[c]
                    sh = 1
                    while sh < win:
                        nxt, tn = ppr()
                        k.tt("pool", nxt[:, sh:L], cur[:, sh:L], cur[:, 0:L - sh], ALU.add, reads=[tcur], writes=[tn])
                        cur, tcur = nxt[:, 0:L], tn
                        sh *= 2
                    if si == 0 and not halo:
                        k.tt("pool", cur[:, ph:ph + 16], cur[:, ph:ph + 16],
                             vec[:, V_PCORR + c * 16:V_PCORR + (c + 1) * 16], ALU.mult, reads=[tcur, t_vec],
                             writes=[tcur])
                    k.stt(hnT[:, c, off:off + n], cur[:, ph:ph + n], 1.0 / win, hnf[:, c, off + ph:off + ph + n],
                          ALU.mult, ALU.subtract, reads=[tcur, t_hnf[c]], writes=[t_hn[c]])
                for g in range(4):
                    for j in range(2):
                        ch = 2 * g + j
                        bk, tb = bank()
                        for kc in range(2):
                            k.mm(bk[:, :n], pw[:, g, kc, j * 128:(j + 1) * 128], hnT[:, 2 * g + kc, off:off + n],
                                 kc == 0, kc == 1, reads=[t_pw, t_hn[2 * g + kc]], writes=[tb])
                        k.stt(hT[:, ch, ph + off:ph + off + n], bk[:, :n], V(V_PSCALE + ch),
                              hT[:, ch, ph + off:ph + off + n], ALU.mult, ALU.add, reads=[tb, t_h[ch], t_vec],
                              writes=[t_h[ch]])
        elif kind == "sb":
            k.dma(big[:, 0:8, 0:NT], oind[:, :, acol:acol + NT], writes=t_big[0:8], eng="pool")
            k.dma(wA[:], wsod[:, :, :], writes=[t_wA])
            proj_add(wA, t_wA, lambda kk, off, n: big[:, kk, off:off + n], t_big, tiles)
        else:
            k.dma(ys[:, :, 0:NT], yind[:, :, acol:acol + NT], writes=[t_ys])
            k.dma(hs[:, :, 0:NT], hnind[:, :, acol:acol + NT], writes=[t_hs])
            for (off, n, halo) in tiles:
                for c in range(8):
                    a, ta = accvr()
                    b, tb_ = accgr()
                    k.stt(a[:, :n], hs[:, c, off:off + n], V(V_S5D + c), ys[:, c, off:off + n], ALU.mult, ALU.add,
                          reads=[t_ys, t_hs, t_vec], writes=[ta])
                    k.act(b[:, :n], a[:, :n], AF.Square, reads=[ta], writes=[tb_])
                    k.ts("dve", b[:, :n], b[:, :n], 0.044715, ALU.mult, 1.0, ALU.add, reads=[tb_], writes=[tb_])
                    k.tt("dve", b[:, :n], b[:, :n], a[:, :n], ALU.mult, reads=[tb_, ta], writes=[tb_])
                    k.act(b[:, :n], b[:, :n], AF.Sigmoid, reads=[tb_], writes=[tb_], scale=1.5957691216057308)
                    k.tt("pool", big[:, c, off:off + n], b[:, :n], a[:, :n], ALU.mult, reads=[tb_, ta],
                         writes=[t_big[c]])
            for grp in range(2):
                k.dma(wA[:], wglud[grp, :, :, :], writes=[t_wA])
                for j in range(4):
                    oc = 4 * grp + j
                    for (off, n, halo) in tiles:
                        bv, tbv = bank()
                        bg, tbg = bank()
                        for kk in range(8):
                            k.mm(bv[:, :n], wA[:, kk, j * 128:(j + 1) * 128], big[:, kk, off:off + n], kk == 0,
                                 kk == 7, reads=[t_wA, t_big[kk]], writes=[tbv])
                        for kk in range(8):
                            k.mm(bg[:, :n], wA[:, kk, 512 + j * 128:512 + (j + 1) * 128], big[:, kk, off:off + n],
                                 kk == 0, kk == 7, reads=[t_wA, t_big[kk]], writes=[tbg])
                        a, ta = accvr()
                        b, tb_ = accgr()
                        k.act(a[:, :n], bg[:, :n], AF.Sigmoid, reads=[tbg], writes=[ta])
                        k.tt("dve", b[:, :n], bv[:, :n], a[:, :n], ALU.mult, reads=[tbv, ta], writes=[tb_])
                        k.tt("pool", hT[:, oc, ph + off:ph + off + n], hT[:, oc, ph + off:ph + off + n], b[:, :n],
                             ALU.add, reads=[tb_, t_h[oc]], writes=[t_h[oc]])

        hsrc = lambda c, lo, n: hT[:, c, ph + lo:ph + lo + n]
        hndst = lambda c, lo, n: hnT[:, c, lo:lo + n]
        rms_rstd(hsrc, NT, t_h)
        normalize(hndst, t_hn, hsrc, t_h, V_XAG, NT)
        k.dma(wA[:], wqd[:, :, :], writes=[t_wA])
        for oc in range(8):
            for (off, n, halo) in tiles:
                bk, tb = bank()
                for kk in range(8):
                    k.mm(bk[:, :n], wA[:, kk, oc * 128:(oc + 1) * 128], hnT[:, kk, off:off + n], kk == 0, kk == 7,
                         reads=[t_wA, t_hn[kk]], writes=[tb])
                k.act(big[:, oc, off:off + n], bk[:, :n], AF.Copy, reads=[tb], writes=[t_big[oc]], scale=1.0 / 16.0)
        k.dma(wA[:], wod[:, :, :], writes=[t_wA])
        for (off, n, halo) in tiles:
            for hd in range(4):
                sb_ = [bank(), bank()]
                for mc in range(2):
                    for dc in range(2):
                        k.mm(sb_[mc][0][:, :n], kT[:, 2 * hd + dc, mc * 128:(mc + 1) * 128],
                             big[:, 2 * hd + dc, off:off + n], dc == 0, dc == 1, reads=[t_kT, t_big[2 * hd + dc]],
                             writes=[sb_[mc][1]])
                et, tet = ETr()
                for mc in range(2):
                    k.act(et[:, mc, :n], sb_[mc][0][:, :n], AF.Exp, reads=[sb_[mc][1]], writes=[tet])
                dn, tdn = bank()
                for mc in range(2):
                    k.mm(dn[:, :n], ones_b[:, :], et[:, mc, :n], mc == 0, mc == 1, reads=[tet, t_ones], writes=[tdn])
                rd, trd = rdenr()
                k.recip(rd[:, :n], dn[:, :n], reads=[tdn], writes=[trd])
                for dvc in range(2):
                    ob, tob = bank()
                    for mc in range(2):
                        k.mm(ob[:, :n], Vm[:, mc, hd * 256 + dvc * 128:hd * 256 + (dvc + 1) * 128], et[:, mc, :n],
                             mc == 0, mc == 1, reads=[t_Vm, tet], writes=[tob])
                    k.tt("dve", big[:, 8 + 2 * hd + dvc, off:off + n], ob[:, :n], rd[:, :n], ALU.mult,
                         reads=[tob, trd], writes=[t_big[8 + 2 * hd + dvc]])
        proj_add(wA, t_wA, lambda kk, off, n: big[:, 8 + kk, off:off + n], t_big[8:16], tiles)

        rms_rstd(hsrc, NT, t_h)
        normalize(hndst, t_hn, hsrc, t_h, V_FFNG, NT)
        for fg in range(11):
            wb, twb = wup[fg % 2], t_wup[fg % 2]
            k.dma(wb[:], wupd[fg, :, :, :], writes=[twb])
            for j in range(2):
                chv = 2 * fg + j
                chg = NFC + 2 * fg + j
                for (off, n, halo) in tiles:
                    bv, tbv = bank()
                    bg, tbg = bank()
                    for kk in range(8):
                        k.mm(bv[:, :n], wb[:, kk, j * 128:(j + 1) * 128], hnT[:, kk, off:off + n], kk == 0, kk == 7,
                             reads=[twb, t_hn[kk]], writes=[tbv])
                    for kk in range(8):
                        k.mm(bg[:, :n], wb[:, kk, 256 + j * 128:256 + (j + 1) * 128], hnT[:, kk, off:off + n],
                             kk == 0, kk == 7, reads=[twb, t_hn[kk]], writes=[tbg])
                    av, tav = accvr()
                    ag, tag = accgr()
                    for (bnk, tbk, ch, acc, tacc) in ((bv, tbv, chv, av, tav), (bg, tbg, chg, ag, tag)):
                        w0, w1, w2 = V(V_CONVW + 3 * ch), V(V_CONVW + 3 * ch + 1), V(V_CONVW + 3 * ch + 2)
                        k.act(acc[:, :n], bnk[:, :n], AF.Identity, reads=[tbk, t_vec], writes=[tacc], scale=w2,
                              bias=V(V_CONVB + ch))
                        k.stt(acc[:, 1:n], bnk[:, 0:n - 1], w1, acc[:, 1:n], ALU.mult, ALU.add, reads=[tbk, tacc],
                              writes=[tacc])
                        k.stt(acc[:, 2:n], bnk[:, 0:n - 2], w0, acc[:, 2:n], ALU.mult, ALU.add, reads=[tbk, tacc],
                              writes=[tacc])
                        k.stt(acc[:, 0:1], carry[:, ch, 1:2], w1, acc[:, 0:1], ALU.mult, ALU.add,
                              reads=[t_carry[ch], tacc], writes=[tacc])
                        k.stt(acc[:, 0:2], carry[:, ch, 0:2], w0, acc[:, 0:2], ALU.mult, ALU.add,
                              reads=[t_carry[ch], tacc], writes=[tacc])
                        if halo:
                            k.ts("dve", carry[:, ch, :], bnk[:, n - 2:n], V(V_FLAG), ALU.mult, reads=[tbk, t_vec],
                                 writes=[t_carry[ch]])
                        else:
                            k.copy("dve", carry[:, ch, :], bnk[:, n - 2:n], reads=[tbk], writes=[t_carry[ch]])
                    sg, tsg = sgbr()
                    k.act(sg[:, :n], ag[:, :n], AF.Silu, reads=[tag], writes=[tsg])
                    k.tt("pool", big[:, chv, off:off + n], sg[:, :n], av[:, :n], ALU.mult, reads=[tsg, tav],
                         writes=[t_big[chv]])
        for oc in range(8):
            wb, twb = wdn[oc % 2], t_wdn[oc % 2]
            k.dma(wb[:], wdnd[oc, :, :, :], writes=[twb])
            for (off, n, halo) in tiles:
                if halo:
                    continue
                bk, tb = bank()
                for f in range(NFC):
                    k.mm(bk[:, :n], wb[:, f, :], big[:, f, off:off + n], f == 0, f == NFC - 1,
                         reads=[twb, t_big[f]], writes=[tb])
                k.tt("dve", hT[:, oc, ph + off:ph + off + n], bk[:, :n], hT[:, oc, ph + off:ph + off + n], ALU.add,
                     reads=[tb, t_h[oc]], writes=[t_h[oc]])

        j0 = 512 * si
        osrc = lambda c, lo, n: hT[:, c, ph + own_off + lo:ph + own_off + lo + n]
        if final:
            rms_rstd(osrc, 512, t_h)
            normalize(lambda c, lo, n: fout[:, c, lo:lo + n], [t_fout] * 8, osrc, t_h, V_FING, 512)
            k.dma(hout[:, :, j0:j0 + 512], fout[:, :, :], reads=[t_fout], is_out=True)
        else:
            k.dma(hout[:, :, j0:j0 + 512], hT[:, :, ph + own_off:ph + own_off + 512], reads=t_h, is_out=True)
        if emit_next:
            rms_rstd(osrc, 512, t_h)
            normalize(lambda c, lo, n: fout[:, c, lo:lo + n], [t_fout] * 8, osrc, t_h, V_NEXTG, 512)
            k.dma(hnout[:, :, j0:j0 + 512], fout[:, :, :], reads=[t_fout], is_out=True)
    return k.finish()


def fm(a):
    T_, C = a.shape
    return np.ascontiguousarray(a.T.reshape(C // 128, 128, T_).transpose(1, 0, 2))


def unfm(o):
    p, ncx, T_ = o.shape
    return np.ascontiguousarray(o.transpose(1, 0, 2).reshape(ncx * p, T_).T)


def kmaj(w):
    K_, N = w.shape
    return np.ascontiguousarray(w.reshape(K_ // 128, 128, N).transpose(1, 0, 2))


def vcol(g):
    return g.reshape(-1, 128).T


def halo_slice(a, hf, pre):
    lo = hf * OWN - pre
    if lo >= 0:
        return a[lo:hf * OWN + OWN]
    pad = np.zeros((-lo, a.shape[1]), a.dtype)
    return np.concatenate([pad, a[0:hf * OWN + OWN]], axis=0)


def layer_weights(i, P, WB):
    w = {}
    w["wq"] = kmaj(WB["xa_wq"][i])
    w["wo"] = kmaj(WB["xa_wo"][i])
    w["wk"] = kmaj(WB["xa_wkv"][i][:, :D])
    w["wv"] = kmaj(WB["xa_wkv"][i][:, D:])
    up = WB["ffn_w_up"][i]
    w["wup"] = np.stack([kmaj(np.concatenate([up[:, fg * 256:(fg + 1) * 256],
                                              up[:, DFF + fg * 256:DFF + (fg + 1) * 256]], axis=1))
                         for fg in range(11)])
    dn = WB["ffn_w_down"][i]
    w["wdn"] = np.stack([kmaj(dn[:, oc * 128:(oc + 1) * 128]) for oc in range(8)])
    return w


def layer_vecs(i, kind, j, P, hf, nxt=None):
    v = np.zeros((128, NV), np.float32)
    v[:, V_MIXG:V_MIXG + 8] = vcol(P["mix_norm_g"][i])
    v[:, V_XAG:V_XAG + 8] = vcol(P["xa_norm_g"][i])
    v[:, V_FFNG:V_FFNG + 8] = vcol(P["ffn_norm_g"][i])
    v[:, V_MEMG:V_MEMG + 8] = vcol(P["mem_norm_g"][i])
    v[:, V_FING:V_FING + 8] = vcol(P["final_norm_g"])
    if nxt is not None:
        v[:, V_NEXTG:V_NEXTG + 8] = vcol(P["mix_norm_g"][nxt])
    if kind == "pool":
        v[:, V_PSCALE:V_PSCALE + 8] = vcol(P["pool_scale"][j])
    if kind == "s5":
        v[:, V_S5D:V_S5D + 8] = vcol(P["s5_d"][j])
    v[:, V_CONVW:V_CONVW + 132] = P["ffn_conv_w"][i].reshape(3, 44, 128).transpose(2, 1, 0).reshape(128, 132)
    v[:, V_CONVB:V_CONVB + 44] = vcol(P["ffn_conv_b"][i])
    v[:, V_FLAG] = float(hf)
    pc = np.ones((8, 16), np.float32)
    if hf == 0:
        for c in range(8):
            win = (2, 4, 8, 16)[c // 2]
            for t in range(16):
                pc[c, t] = win / min(t + 1, win)
    v[:, V_PCORR:V_PCORR + 128] = pc.reshape(1, 128)
    return v


def run_layer(i, kind, j, h, mem, P, WB, final=False, nxt=None, extra=None):
    nc = _get(("layer", kind, final, nxt is not None), lambda: build_layer(kind, final, nxt is not None))
    ph = PH if kind == "pool" else 0
    lw = layer_weights(i, P, WB)
    if kind == "pool":
        lw["pw"] = np.ascontiguousarray(WB["pool_w"][j].reshape(4, 2, 128, 256).transpose(2, 0, 1, 3))
    elif kind == "sb":
        lw["wso"] = kmaj(WB["sb_w_o"][j])
    else:
        g = WB["s5_w_glu"][j]
        lw["wglu"] = np.stack([kmaj(np.concatenate([g[:, grp * 512:(grp + 1) * 512],
                                                    g[:, D + grp * 512:D + (grp + 1) * 512]], axis=1))
                               for grp in range(2)])
    in_maps = []
    for c in range(NCORES):
        b, hf = c // 2, c % 2
        m = dict(lw)
        m["hin"] = fm(halo_slice(h[b], hf, HT + ph))
        m["vecs"] = layer_vecs(i, kind, j, P, hf, nxt)
        m["memT"] = fm(mem[b])
        if kind == "sb":
            m["oin"] = fm(halo_slice(extra["o"][b], hf, HT))
        elif kind == "s5":
            m["yin"] = fm(halo_slice(extra["y"][b], hf, HT))
            m["hnin"] = fm(halo_slice(extra["hn"][b], hf, HT))
        in_maps.append(m)
    res = run_bass_kernel_spmd(nc, in_maps, core_ids=list(range(NCORES)))
    hnew = np.empty_like(h)
    hn = np.empty_like(h) if nxt is not None else None
    for c in range(NCORES):
        b, hf = c // 2, c % 2
        hnew[b, hf * OWN:(hf + 1) * OWN] = unfm(np.asarray(res.results[c]["hout"]))
        if hn is not None:
            hn[b, hf * OWN:(hf + 1) * OWN] = unfm(np.asarray(res.results[c]["hnout"]))
    return hnew, hn


WNAMES = ["pool_w", "sb_w_qkv", "sb_w_o", "s5_w_glu", "xa_wq", "xa_wkv", "xa_wo", "ffn_w_up", "ffn_w_down"]


def cast_weights(P):
    outs = run_cast([P[n] for n in WNAMES])
    return dict(zip(WNAMES, outs))


def build_sbcore():
    k = KB()
    hin = k.dram("hin", [128, 8, SEQ], F32)
    vecd = k.dram("vecs", [128, 8], F32)
    wqd = k.dram("wq", [128, 8, 512], BF16)
    wkd = k.dram("wk", [128, 8, 512], BF16)
    wvd = k.dram("wv", [128, 8, 512], BF16)
    cmd = k.dram("cmask", [128, 2, 128], BF16)
    oout = k.dram("oout", [128, 4, SEQ], F32, out=True)

    qT = k.sb("qT", [128, 4, SEQ], BF16)
    kT = k.sb("kT", [128, 4, SEQ], BF16)
    Vt = k.sb("Vt", [128, 32, 512], BF16)
    hT = k.sb("hT", [128, 8, 512], F32)
    hnT = k.sb("hnT", [128, 8, 512], BF16)
    wq = k.sb("wq_sb", [128, 8, 512], BF16)
    wk = k.sb("wk_sb", [128, 8, 512], BF16)
    wv = k.sb("wv_sb", [128, 8, 512], BF16)
    vec = k.sb("vec", [128, 8], F32)
    cm = k.sb("cm", [128, 2, 128], BF16)
    ones_f = k.sb("ones_f", [128, 128], F32)
    negones = k.sb("negones", [128, 128], BF16)
    onec = k.sb("onec", [128, 1], F32)
    epsb = k.sb("epsb", [128, 1], F32)
    rstd = k.sb("rstd", [128, 512], F32)
    sq = [k.sb(f"sq{i}", [128, 512], F32) for i in range(2)]
    ez = [k.sb(f"ez{i}", [128, 512], F32) for i in range(2)]
    sp = [k.sb(f"sp{i}", [128, 512], BF16) for i in range(2)]
    ar = [k.sb(f"ar{i}", [128, 512], F32) for i in range(2)]
    Ab = [k.sb(f"Ab{i}", [128, 512], BF16) for i in range(2)]
    Rb = [k.sb(f"Rb{i}", [128, 512], F32) for i in range(2)]
    ost = [k.sb(f"ost{i}", [128, 512], F32) for i in range(2)]
    banks = [k.ps(f"bank{i}") for i in range(8)]

    t_q = [T(f"q{c}") for c in range(4)]
    t_k = [T(f"k{c}") for c in range(4)]
    t_V, t_h, t_hn, t_w, t_vec, t_c, t_rstd = T("V"), T("h"), T("hn"), T("w"), T("vec"), T("c"), T("rstd")
    bank = Rot([(banks[i], T(f"bank{i}")) for i in range(6)])
    obank = Rot([(banks[6 + i], T(f"obank{i}")) for i in range(2)])
    sqr = Rot([(sq[i], T(f"sq{i}")) for i in range(2)])
    ezr = Rot([(ez[i], T(f"ez{i}")) for i in range(2)])
    spr = Rot([(sp[i], T(f"sp{i}")) for i in range(2)])
    arr = Rot([(ar[i], T(f"ar{i}")) for i in range(2)])
    Abr = Rot([(Ab[i], T(f"Ab{i}")) for i in range(2)])
    Rbr = Rot([(Rb[i], T(f"Rb{i}")) for i in range(2)])
    ostr = Rot([(ost[i], T(f"ost{i}")) for i in range(2)])

    k.memset("pool", ones_f[:], 1.0, writes=[t_c])
    k.memset("pool", negones[:], -1.0, writes=[t_c])
    k.memset("pool", onec[:], 1.0, writes=[t_c])
    k.memset("pool", epsb[:], EPS, writes=[t_c])
    k.dma(vec[:], vecd[:, :], writes=[t_vec])
    k.dma(cm[:], cmd[:, :, :], writes=[t_c])
    k.dma(wq[:], wqd[:, :, :], writes=[t_w])
    k.dma(wk[:], wkd[:, :, :], writes=[t_w])
    k.dma(wv[:], wvd[:, :, :], writes=[t_w])

    for tt_ in range(8):
        c0 = tt_ * 512
        k.dma(hT[:], hin[:, :, c0:c0 + 512], writes=[t_h])
        bk, tb = bank()
        for c in range(8):
            s, ts_ = sqr()
            k.act(s[:], hT[:, c, :], AF.Square, reads=[t_h], writes=[ts_])
            k.mm(bk[:, :], ones_f[:, :], s[:], c == 0, c == 7, reads=[ts_, t_c], writes=[tb])
        k.act(rstd[:], bk[:, :], AF.Sqrt, reads=[tb, t_c], writes=[t_rstd], scale=1.0 / D, bias=epsb[:, 0:1])
        k.recip(rstd[:], rstd[:], reads=[t_rstd], writes=[t_rstd])
        for c in range(8):
            k.stt(hnT[:, c, :], hT[:, c, :], vec[:, c:c + 1], rstd[:], ALU.mult, ALU.mult,
                  reads=[t_h, t_rstd, t_vec], writes=[t_hn])
        for oc in range(4):
            bk, tb = bank()
            for kk in range(8):
                k.mm(bk[:, :], wq[:, kk, oc * 128:(oc + 1) * 128], hnT[:, kk, :], kk == 0, kk == 7,
                     reads=[t_w, t_hn], writes=[tb])
            k.act(qT[:, oc, c0:c0 + 512], bk[:, :], AF.Copy, reads=[tb], writes=[t_q[oc]], scale=0.125)
            bk, tb = bank()
            for kk in range(8):
                k.mm(bk[:, :], wk[:, kk, oc * 128:(oc + 1) * 128], hnT[:, kk, :], kk == 0, kk == 7,
                     reads=[t_w, t_hn], writes=[tb])
            k.copy("dve", kT[:, oc, c0:c0 + 512], bk[:, :], reads=[tb], writes=[t_k[oc]])
        for tc in range(4):
            bk, tb = bank()
            for kk in range(8):
                k.mm(bk[:, :], hnT[:, kk, tc * 128:(tc + 1) * 128], wv[:, kk, :], kk == 0, kk == 7,
                     reads=[t_w, t_hn], writes=[tb])
            k.copy("act" if tc % 2 else "dve", Vt[:, tt_ * 4 + tc, :], bk[:, :], reads=[tb], writes=[t_V])

    for hc in range(4):
        for qt in range(8):
            os_, tos = ostr()
            for e in range(2):
                pr = slice(e * 64, (e + 1) * 64)
                R, tR = Rbr()
                k.memset("pool", R[:], 0.0, writes=[tR])
                ob, tob = obank()
                first = True
                for kc in range(4 * qt + 3, -1, -1):
                    jd = kc - 4 * qt
                    diag = jd >= 0
                    q_lo = 128 * jd if diag else 0
                    n = 512 - q_lo
                    zb, tz = bank()
                    k.mm(zb[:, :n], kT[pr, hc, kc * 128:(kc + 1) * 128], qT[pr, hc, qt * 512 + q_lo:(qt + 1) * 512],
                         True, False, reads=[t_k[hc], t_q[hc]], writes=[tz])
                    ezb, tez = ezr()
                    k.act(ezb[:, :n], zb[:, :n], AF.Exp, reads=[tz], writes=[tez])
                    spb, tsp = spr()
                    k.act(spb[:, :n], ezb[:, :n], AF.Ln, reads=[tez, t_c], writes=[tsp], bias=onec[:, 0:1])
                    if diag:
                        k.tt("pool", spb[:, 0:128], spb[:, 0:128], cm[:, 0, :], ALU.mult, reads=[tsp, t_c],
                             writes=[tsp])
                    k.mm(zb[:, :n], cm[:, 1, :], spb[:, :n], False, True, reads=[tsp, t_c], writes=[tz])
                    if kc > 0:
                        cb, tcb = bank()
                        k.mm(cb[:, :n], negones[:, :], spb[:, :n], True, True, reads=[tsp, t_c], writes=[tcb])
                    arb, tar = arr()
                    k.tt("dve", arb[:, :n], zb[:, :n], R[:, q_lo:512], ALU.add, reads=[tz, tR], writes=[tar])
                    if kc > 0:
                        k.tt("dve", R[:, q_lo:512], cb[:, :n], R[:, q_lo:512], ALU.add, reads=[tcb, tR], writes=[tR])
                    Abb, tAb = Abr()
                    k.act(Abb[:, :n], arb[:, :n], AF.Exp, reads=[tar], writes=[tAb])
                    if diag:
                        k.tt("pool", Abb[:, 0:128], Abb[:, 0:128], cm[:, 0, :], ALU.mult, reads=[tAb, t_c],
                             writes=[tAb])
                    k.mm(ob[pr, q_lo:512], Vt[:, kc, (hc * 2 + e) * 64:(hc * 2 + e + 1) * 64], Abb[:, :n], first,
                         kc == 0, reads=[tAb, t_V], writes=[tob])
                    first = False
                k.copy("act", os_[pr, :], ob[pr, :], reads=[tob], writes=[tos])
            k.dma(oout[:, hc, qt * 512:(qt + 1) * 512], os_[:, :], reads=[tos], is_out=True)
    return k.finish()


def run_sbcore(h, P, WB, i=1, j=0):
    nc = _get(("sbcore",), build_sbcore)
    wqkv = WB["sb_w_qkv"][j]
    kk, jj = np.meshgrid(np.arange(128), np.arange(128), indexing="ij")
    cmask = np.zeros((128, 2, 128), np.float32)
    cmask[:, 0, :] = (kk < jj)
    cmask[:, 1, :] = -1.0 * (kk >= jj)
    cmask = cmask.astype(NPBF)
    in_maps = []
    for c in range(NCORES):
        b, hf = c // 2, c % 2
        sl = slice(hf * 512, (hf + 1) * 512)
        in_maps.append({
            "hin": fm(h[b]),
            "vecs": np.ascontiguousarray(vcol(P["mix_norm_g"][i])),
            "wq": kmaj(wqkv[:, 0:D][:, sl]),
            "wk": kmaj(wqkv[:, D:2 * D][:, sl]),
            "wv": kmaj(wqkv[:, 2 * D:3 * D][:, sl]),
            "cmask": cmask,
        })
    res = run_bass_kernel_spmd(nc, in_maps, core_ids=list(range(NCORES)))
    o = np.empty((BATCH, SEQ, D), np.float32)
    for c in range(NCORES):
        b, hf = c // 2, c % 2
        o[b][:, hf * 512:(hf + 1) * 512] = unfm(np.asarray(res.results[c]["oout"]))
    return o


NJ = SEQ // 8
F_ARE, F_AIM, F_LDT, F_BRE, F_BIM, F_CRE, F_CIM, NF = 0, 1, 2, 3, 19, 35, 51, 67
TWO_PI = 6.283185307179586
MAGIC = 12582912.0


def build_s5core():
    k = KB()
    zd = k.dram("z", [128, 32, NJ], F32)
    pd = k.dram("s5p", [128, NF, 16], F32)
    cd = k.dram("consts", [128, 192], F32)
    yout = k.dram("yout", [32, 128, NJ], F32, out=True)

    Z = k.sb("Z", [128, 32, NJ], BF16)
    prm = k.sb("prm", [128, NF, 16], F32)
    cst = k.sb("cst", [128, 192], F32)
    tmp = k.sb("tmp", [128, 48, 16], F32)
    Bbr = k.sb("Bbr", [128, 16, 16], F32)
    Bbi = k.sb("Bbi", [128, 16, 16], F32)
    Cr = k.sb("Cr", [128, 16, 16], F32)
    Ci = k.sb("Ci", [128, 16, 16], F32)
    pwr = k.sb("pwr", [128, 9, 16], F32)
    pwi = k.sb("pwi", [128, 9, 16], F32)
    ipr = k.sb("ipr", [128, 8, 16], F32)
    ipi = k.sb("ipi", [128, 8, 16], F32)
    LA = k.sb("LA", [128, 16, 8, 16], F32)
    LB = k.sb("LB", [128, 16, 8, 16], F32)
    RTr = k.sb("RTr", [128, 16, 9, 16], F32)
    nRTi = k.sb("nRTi", [128, 16, 9, 16], F32)
    t3a = k.sb("t3a", [128, 16, 16], F32)
    t3b = k.sb("t3b", [128, 16, 16], F32)
    CxR = k.sb("CxR", [128, 16, 128], BF16)
    CxI = k.sb("CxI", [128, 16, 128], BF16)
    Tsb = k.sb("Tsb", [128, 32, 128], BF16)
    BxT = k.sb("BxT", [128, 32, 2, 64], BF16)
    BJ = k.sb("BJ", [128, 2, 16, NJ], F32)
    Sbf = k.sb("Sbf", [128, 2, 16, NJ], BF16)
    X1 = k.sb("X1", [128, 2, 16], F32)
    AIp = k.sb("AIp", [128, 16], F32)
    AIn = k.sb("AIn", [128, 16], F32)
    u1 = [k.sb(f"u1{i}", [128, 2, 8], F32) for i in range(2)]
    u2 = [k.sb(f"u2{i}", [128, 2, 8], F32) for i in range(2)]
    yst = [k.sb(f"yst{i}", [128, NJ], F32) for i in range(2)]
    cols = k.sb("cols", [128, 4], F32)
    banks = [k.ps(f"bank{i}") for i in range(8)]
    bank = Rot([(banks[i], T(f"bank{i}")) for i in range(8)])
    ystr = Rot([(yst[i], T(f"yst{i}")) for i in range(2)])

    t_Z, t_prm, t_cst, t_drv = T("Z"), T("prm"), T("cst"), T("drv")
    t_T, t_Bx, t_Cx = T("Tsb"), T("BxT"), T("Cx")
    t_LA, t_LB, t_RT = T("LA"), T("LB"), T("RT")
    t_BJ = [T(f"BJ{h}") for h in range(2)]
    t_S = [T(f"S{h}") for h in range(2)]
    t_u = [T("u_dve"), T("u_pool")]

    k.dma(Z[:, 0:16, :], zd[:, 0:16, :], writes=[t_Z], eng="pool")
    k.dma(Z[:, 16:32, :], zd[:, 16:32, :], writes=[t_Z], eng="pool")
    k.dma(prm[:], pd[:, :, :], writes=[t_prm])
    k.dma(cst[:], cd[:, :], writes=[t_cst])
    k.memset("pool", cols[:, 0:1], 1.0, writes=[t_cst])
    k.memset("pool", cols[:, 1:2], TWO_PI / 4.0, writes=[t_cst])
    k.memset("pool", Sbf[:, :, :, 0:1], 0.0, writes=t_S)

    R_ = [t_prm, t_cst, t_drv]
    W_ = [t_drv]
    tmpi = [0]

    def tm():
        i = tmpi[0]
        tmpi[0] += 1
        return tmp[:, i, :]

    def TT(o, a, b, op):
        k.tt("dve", o, a, b, op, reads=R_, writes=W_)

    def mul(o, a, b):
        TT(o, a, b, ALU.mult)

    def add(o, a, b):
        TT(o, a, b, ALU.add)

    def sub(o, a, b):
        TT(o, a, b, ALU.subtract)

    def ACT(o, a, f, **kw):
        k.act(o, a, f, reads=R_, writes=W_, **kw)

    def cmul(orr, oi, ar, ai, br, bi, s1, s2, neg_im=False):
        mul(s1, ar, br)
        mul(s2, ai, bi)
        sub(orr, s1, s2)
        mul(s1, ar, bi)
        mul(s2, ai, br)
        if neg_im:
            k.stt(oi, s1, -1.0, s2, ALU.mult, ALU.subtract, reads=R_, writes=W_)
        else:
            add(oi, s1, s2)

    a_re, a_im, ldt = prm[:, F_ARE, :], prm[:, F_AIM, :], prm[:, F_LDT, :]
    dt, dlr, dli, mag, kf, r_, sinv, cosv, absr = tm(), tm(), tm(), tm(), tm(), tm(), tm(), tm(), tm()
    abr, abi, xr, den, inv, wr, wi, s1, s2 = tm(), tm(), tm(), tm(), tm(), tm(), tm(), tm(), tm()
    ACT(dt, ldt, AF.Exp)
    mul(dlr, a_re, dt)
    mul(dli, a_im, dt)
    ACT(mag, dlr, AF.Exp)
    k.ts("dve", kf, dli, 1.0 / TWO_PI, ALU.mult, MAGIC, ALU.add, reads=R_, writes=W_)
    k.ts("dve", kf, kf, -MAGIC, ALU.add, reads=R_, writes=W_)
    k.stt(r_, kf, -TWO_PI, dli, ALU.mult, ALU.add, reads=R_, writes=W_)
    k.ts("dve", r_, r_, TWO_PI / 2.0, ALU.min, -TWO_PI / 2.0, ALU.max, reads=R_, writes=W_)
    ACT(sinv, r_, AF.Sin)
    k.ts("dve", absr, r_, -1.0, ALU.mult, reads=R_, writes=W_)
    TT(absr, absr, r_, ALU.max)
    ACT(cosv, absr, AF.Sin, scale=-1.0, bias=cols[:, 1:2])
    mul(abr, mag, cosv)
    mul(abi, mag, sinv)
    k.ts("dve", xr, abr, -1.0, ALU.add, reads=R_, writes=W_)
    mul(s1, a_re, a_re)
    mul(s2, a_im, a_im)
    add(den, s1, s2)
    k.recip(inv, den, reads=R_, writes=W_)
    mul(s1, xr, a_re)
    mul(s2, abi, a_im)
    add(wr, s1, s2)
    mul(wr, wr, inv)
    mul(s1, abi, a_re)
    mul(s2, xr, a_im)
    sub(wi, s1, s2)
    mul(wi, wi, inv)

    def bc(x):
        return x.unsqueeze(2).broadcast_to([128, 16, 16])

    b_re = prm[:, F_BRE:F_BRE + 16, :].rearrange("p c g -> p g c")
    b_im = prm[:, F_BIM:F_BIM + 16, :].rearrange("p c g -> p g c")
    c_re = prm[:, F_CRE:F_CRE + 16, :].rearrange("p c g -> p g c")
    c_im = prm[:, F_CIM:F_CIM + 16, :].rearrange("p c g -> p g c")
    cmul(Bbr[:], Bbi[:], b_re, b_im, bc(wr), bc(wi), t3a[:], t3b[:])
    k.copy("dve", Cr[:], c_re, reads=R_, writes=W_)
    k.copy("dve", Ci[:], c_im, reads=R_, writes=W_)
    k.memset("pool", pwr[:, 0, :], 1.0, writes=W_)
    k.memset("pool", pwi[:, 0, :], 0.0, writes=W_)
    k.memset("pool", ipr[:, 0, :], 1.0, writes=W_)
    k.memset("pool", ipi[:, 0, :], 0.0, writes=W_)
    for d in range(1, 9):
        cmul(pwr[:, d, :], pwi[:, d, :], pwr[:, d - 1, :], pwi[:, d - 1, :], abr, abi, s1, s2)
    n2, in2, air, aii = tm(), tm(), tm(), tm()
    mul(s1, abr, abr)
    mul(s2, abi, abi)
    add(n2, s1, s2)
    k.recip(in2, n2, reads=R_, writes=W_)
    mul(air, abr, in2)
    k.stt(aii, abi, -1.0, in2, ALU.mult, ALU.mult, reads=R_, writes=W_)
    for s in range(1, 8):
        cmul(ipr[:, s, :], ipi[:, s, :], ipr[:, s - 1, :], ipi[:, s - 1, :], air, aii, s1, s2)
    k.copy("dve", X1[:, 0, :], pwr[:, 8, :], reads=R_, writes=W_)
    k.copy("dve", X1[:, 1, :], pwr[:, 8, :], reads=R_, writes=W_)
    k.copy("dve", AIp[:], pwi[:, 8, :], reads=R_, writes=W_)
    k.ts("dve", AIn[:], pwi[:, 8, :], -1.0, ALU.mult, reads=R_, writes=W_)
    for s in range(8):
        cmul(LA[:, :, s, :], LB[:, :, s, :], Bbr[:], Bbi[:], bc(ipr[:, s, :]), bc(ipi[:, s, :]), t3a[:], t3b[:])
    for t in range(9):
        cmul(RTr[:, :, t, :], nRTi[:, :, t, :], Cr[:], Ci[:], bc(pwr[:, t, :]), bc(pwi[:, t, :]), t3a[:], t3b[:],
             neg_im=True)
    DR = [t_drv, t_cst]
    for g in range(32):
        gp, par = g // 2, g % 2
        pr = slice(par * 64, par * 64 + 64)
        bk, tb = bank()
        k.mm(bk[:, :128], LA[pr, gp, :, :].rearrange("p s c -> p (s c)"),
             RTr[pr, gp, 0:8, :].rearrange("p t i -> p (t i)"), True, False, reads=DR, writes=[tb])
        k.mm(bk[:, :128], LB[pr, gp, :, :].rearrange("p s c -> p (s c)"),
             nRTi[pr, gp, 0:8, :].rearrange("p t i -> p (t i)"), False, True, reads=DR, writes=[tb])
        k.tt("dve", Tsb[:, g, :], bk[:, :128], cst[:, 0:128], ALU.mult, reads=[tb, t_cst], writes=[t_T])
    k.copy("act", CxR[:], RTr[:, :, 1:9, :].rearrange("p g t i -> p g (t i)"), reads=DR, writes=[t_Cx])
    k.copy("act", CxI[:], nRTi[:, :, 1:9, :].rearrange("p g t i -> p g (t i)"), reads=DR, writes=[t_Cx])
    R2 = [t_drv, t_cst]
    for s in range(8):
        mul(t3a[:], Bbr[:], bc(pwr[:, 7 - s, :]))
        mul(t3b[:], Bbi[:], bc(pwi[:, 7 - s, :]))
        k.tt("dve", LA[:, :, s, :], t3a[:], t3b[:], ALU.subtract, reads=R_ + [t_T], writes=W_)
        mul(t3a[:], Bbi[:], bc(pwr[:, 7 - s, :]))
        mul(t3b[:], Bbr[:], bc(pwi[:, 7 - s, :]))
        k.tt("dve", LB[:, :, s, :], t3a[:], t3b[:], ALU.add, reads=R_ + [t_T], writes=W_)
    for g in range(32):
        gp, par = g // 2, g % 2
        pr = slice(par * 64, par * 64 + 64)
        for ri, tab in ((0, LA), (1, LB)):
            bk, tb = bank()
            k.mm(bk[:, :64], tab[pr, gp, :, :].rearrange("p s c -> p (s c)"), cst[pr, 128:192], True, True,
                 reads=R2, writes=[tb])
            k.copy("act" if ri else "dve", BxT[:, g, ri, :], bk[:, :64], reads=[tb], writes=[t_Bx])

    for gp in range(16):
        bre, tbre = bank()
        bim, tbim = bank()
        for par in range(2):
            g = 2 * gp + par
            pr = slice(par * 64, par * 64 + 64)
            k.mm(bre[pr, :], BxT[:, g, 0, :], Z[:, g, :], True, True, reads=[t_Bx, t_Z], writes=[tbre])
            k.mm(bim[pr, :], BxT[:, g, 1, :], Z[:, g, :], True, True, reads=[t_Bx, t_Z], writes=[tbim])
        k.copy("act", BJ[:, 0, gp, :], bre[:, :], reads=[tbre], writes=[t_BJ[gp // 8]])
        k.copy("dve", BJ[:, 1, gp, :], bim[:, :], reads=[tbim], writes=[t_BJ[gp // 8]])

    for j in range(1, NJ):
        for h, eng in ((0, "dve"), (1, "pool")):
            G8 = slice(8 * h, 8 * h + 8)
            cur = BJ[:, :, G8, j - 1]
            a1, a2 = u1[h], u2[h]
            rd = [t_BJ[h], t_u[h], t_drv]
            k.tt(eng, a1[:], X1[:, :, G8], cur, ALU.mult, reads=rd, writes=[t_u[h]])
            k.tt(eng, a2[:, 0, :], AIn[:, G8], BJ[:, 1, G8, j - 1], ALU.mult, reads=rd, writes=[t_u[h]])
            k.tt(eng, a2[:, 1, :], AIp[:, G8], BJ[:, 0, G8, j - 1], ALU.mult, reads=rd, writes=[t_u[h]])
            k.tt(eng, a1[:], a1[:], a2[:], ALU.add, reads=rd, writes=[t_u[h]])
            k.tt(eng, BJ[:, :, G8, j], a1[:], BJ[:, :, G8, j], ALU.add, reads=rd, writes=[t_BJ[h]])
    for h in range(2):
        G8 = slice(8 * h, 8 * h + 8)
        k.copy("act", Sbf[:, 0, G8, 1:NJ], BJ[:, 0, G8, 0:NJ - 1], reads=[t_BJ[h]], writes=[t_S[h]])
        k.copy("act" if h else "dve", Sbf[:, 1, G8, 1:NJ], BJ[:, 1, G8, 0:NJ - 1], reads=[t_BJ[h]], writes=[t_S[h]])

    for g in range(32):
        gp, par = g // 2, g % 2
        pr = slice(par * 64, par * 64 + 64)
        bk, tb = bank()
        k.mm(bk[:, :], Tsb[:, g, :], Z[:, g, :], True, False, reads=[t_T, t_Z], writes=[tb])
        k.mm(bk[:, :], CxR[pr, gp, :], Sbf[pr, 0, gp, :], False, False, reads=[t_Cx, t_S[gp // 8]], writes=[tb])
        k.mm(bk[:, :], CxI[pr, gp, :], Sbf[pr, 1, gp, :], False, True, reads=[t_Cx, t_S[gp // 8]], writes=[tb])
        ys_, tys = ystr()
        k.copy("act" if g % 2 else "dve", ys_[:], bk[:, :], reads=[tb], writes=[tys])
        k.dma(yout[g, :, :], ys_[:], reads=[tys], is_out=True)
    return k.finish()


def run_s5core(hn, P, j=0):
    nc = _get(("s5core",), build_s5core)
    consts = np.zeros((128, 192), np.float32)
    s_idx = np.arange(128) // 16
    consts[:, 0:128] = (s_idx[None, :] >= s_idx[:, None])
    consts[:, 128:192] = np.tile(np.eye(64, dtype=np.float32), (2, 1))
    in_maps = []
    for c in range(NCORES):
        b, hf = c // 2, c % 2
        gs = slice(32 * hf, 32 * hf + 32)
        z = hn[b].reshape(NJ, 8, 64, 16)[:, :, gs, :].transpose(1, 3, 2, 0).reshape(128, 32, NJ)

        def pg(a):
            return a.reshape(16, 2, 64).transpose(1, 2, 0).reshape(128, 16)

        def pgc(a):
            return a.reshape(16, 2, 64, 16).transpose(1, 2, 3, 0).reshape(128, 16, 16)

        s5p = np.zeros((128, NF, 16), np.float32)
        s5p[:, F_ARE] = pg(P["s5_a_re"][j][gs])
        s5p[:, F_AIM] = pg(P["s5_a_im"][j][gs])
        s5p[:, F_LDT] = pg(np.repeat(P["s5_log_dt"][j][gs][:, None], 64, axis=1))
        s5p[:, F_BRE:F_BRE + 16] = pgc(P["s5_b_re"][j][gs])
        s5p[:, F_BIM:F_BIM + 16] = pgc(P["s5_b_im"][j][gs])
        s5p[:, F_CRE:F_CRE + 16] = pgc(P["s5_c_re"][j][gs].transpose(0, 2, 1))
        s5p[:, F_CIM:F_CIM + 16] = pgc(P["s5_c_im"][j][gs].transpose(0, 2, 1))
        in_maps.append({"z": np.ascontiguousarray(z), "s5p": s5p, "consts": consts})
    res = run_bass_kernel_spmd(nc, in_maps, core_ids=list(range(NCORES)))
    y = np.empty((BATCH, SEQ, D), np.float32)
    for c in range(NCORES):
        b, hf = c // 2, c % 2
        yo = np.asarray(res.results[c]["yout"])
        y[b].reshape(NJ, 8, 64, 16)[:, :, 32 * hf:32 * hf + 32, :] = yo.reshape(32, 8, 16, NJ).transpose(3, 1, 0, 2)
    return y


def kernel(**inputs):
    P = {k_: np.ascontiguousarray(np.asarray(v, dtype=np.float32)) for k_, v in inputs.items()}
    mem = P["mem"]
    WB = cast_weights(P)
    h = P["x"]
    h, _ = run_layer(0, "pool", 0, h, mem, P, WB)
    o = run_sbcore(h, P, WB, i=1, j=0)
    h, hn2 = run_layer(1, "sb", 0, h, mem, P, WB, nxt=2, extra={"o": o})
    y = run_s5core(hn2, P, j=0)
    h, _ = run_layer(2, "s5", 0, h, mem, P, WB, extra={"y": y, "hn": hn2})
    out, _ = run_layer(3, "pool", 1, h, mem, P, WB, final=True)
    return out.astype(np.float32)
```
